# Optimizing a Trainium2 kernel written in Bass

```python
import math
import jax
import jax.numpy as jnp
from jax import lax
import numpy as np

D_MODEL = 1024
BATCH = 8
SEQ = 4096
DEPTH = 4

GRID_W = 64
CTX_LEN = 256
N_MOD = 9
D_FF = 256 * ((8 * D_MODEL // 3 + 255) // 256)
DN_HEAD_DIM = 128
DN_WIDTH = D_MODEL // 2
DN_HEADS = DN_WIDTH // DN_HEAD_DIM
CONV_W = 3
CHUNK = 64
POOL_WINDOWS = (2, 4, 8, 16)
POOL_GROUPS = len(POOL_WINDOWS)
POOL_WIDTH = D_MODEL // 4
POOL_GROUP_DIM = POOL_WIDTH // POOL_GROUPS
FOURIER_GROUPS = 4
FOURIER_WIDTH = D_MODEL // 4
FOURIER_GROUP_DIM = FOURIER_WIDTH // FOURIER_GROUPS
N_BRANCH = 3
PROJ_SIZES = (3 * DN_WIDTH, DN_WIDTH, 2 * DN_HEADS, 2 * DN_HEADS, POOL_WIDTH, FOURIER_WIDTH, N_BRANCH * D_MODEL)
D_PROJ = sum(PROJ_SIZES)
RMS_EPS = 1e-6
L2_EPS = 1e-6

kernel_name = 'hybrid_deltanet_pool_fourier_prefix_dit'


def _rms(x, g):
    xf = x.astype(jnp.float32)
    y = xf * lax.rsqrt(jnp.mean(xf * xf, axis=-1, keepdims=True) + RMS_EPS)
    return (y * g.astype(jnp.float32)).astype(x.dtype)


def _half_ffn(x, gain, shift, scale, gate, w_gu, w_down):
    h = _rms(x, gain) * (1 + scale) + shift
    a, u = jnp.split(h @ w_gu, 2, axis=-1)
    return x + 0.5 * gate * ((jax.nn.silu(a) * u) @ w_down)


def _short_conv(u, w):
    pad = CONV_W // 2
    t = u.shape[1]
    up = jnp.pad(u, ((0, 0), (pad, pad), (0, 0)))
    return sum(up[:, j:j + t] * w[j] for j in range(CONV_W))


def _l2n(a):
    return a * lax.rsqrt(jnp.sum(a * a, axis=-1, keepdims=True) + L2_EPS)


def _project(h, w_in, conv_w):
    b, t, _ = h.shape
    cuts = [sum(PROJ_SIZES[:i + 1]) for i in range(len(PROJ_SIZES) - 1)]
    qkv, z, beta_raw, alpha_raw, pool_in, four_in, gates = jnp.split(h @ w_in, cuts, axis=-1)
    qkv = jax.nn.silu(_short_conv(qkv, conv_w)).astype(jnp.float32)
    qkv = qkv.reshape(b, t, 3, DN_HEADS, DN_HEAD_DIM)
    q = _l2n(qkv[:, :, 0]) * (DN_HEAD_DIM ** -0.5)
    k = _l2n(qkv[:, :, 1])
    v = qkv[:, :, 2]
    return q, k, v, beta_raw, alpha_raw, z, pool_in, four_in, gates


def _decay(beta_raw, alpha_raw, a_log_d, dt_bias_d, d):
    sl = slice(d * DN_HEADS, (d + 1) * DN_HEADS)
    beta = jax.nn.sigmoid(beta_raw[..., sl].astype(jnp.float32))
    g = -jnp.exp(a_log_d.astype(jnp.float32)) * jax.nn.softplus(
        alpha_raw[..., sl].astype(jnp.float32) + dt_bias_d.astype(jnp.float32))
    return beta, g


def _chunk_gated_delta(q, k, v, beta, g, s0):
    b, t, h, _ = q.shape
    dv = v.shape[-1]
    n = t // CHUNK

    def blocks(a):
        a = a.reshape((b, n, CHUNK, h) + a.shape[3:])
        return jnp.moveaxis(a, (1, 3), (0, 2))

    qb, kb, vb, betab, gb = blocks(q), blocks(k), blocks(v), blocks(beta), blocks(g)
    gc = jnp.cumsum(gb, axis=-1)
    idx = jnp.arange(CHUNK)
    incl = idx[:, None] >= idx[None, :]
    strict = idx[:, None] > idx[None, :]
    decay = jnp.exp(jnp.where(incl, gc[..., :, None] - gc[..., None, :], -jnp.inf))
    kbeta = kb * betab[..., None]
    a_mat = jnp.where(strict, jnp.einsum('nbhik,nbhjk->nbhij', kbeta, kb) * decay, 0.0)
    eye = jnp.eye(CHUNK, dtype=jnp.float32)
    rhs = jnp.concatenate([vb * betab[..., None], kbeta * jnp.exp(gc)[..., None]], axis=-1)
    sol = lax.linalg.triangular_solve(eye + a_mat, rhs, left_side=True, lower=True, unit_diagonal=True)
    u, w = sol[..., :dv], sol[..., dv:]
    attn = jnp.einsum('nbhik,nbhjk->nbhij', qb, kb) * decay
    q_dec = qb * jnp.exp(gc)[..., None]
    k_tail = kb * jnp.exp(gc[..., -1:] - gc)[..., None]
    c_dec = jnp.exp(gc[..., -1])

    def step(s, xs):
        u_i, w_i, a_i, qd_i, kt_i, cd_i = xs
        v_new = u_i - jnp.einsum('bhck,bhkv->bhcv', w_i, s)
        o_i = jnp.einsum('bhck,bhkv->bhcv', qd_i, s) + jnp.einsum('bhij,bhjv->bhiv', a_i, v_new)
        s = s * cd_i[..., None, None] + jnp.einsum('bhck,bhcv->bhkv', kt_i, v_new)
        return s, o_i

    s_fin, o = lax.scan(step, s0, (u, w, attn, q_dec, k_tail, c_dec))
    o = jnp.moveaxis(o, (0, 2), (1, 3)).reshape(b, t, h, dv)
    return o, s_fin


def _scan_dir(q, k, v, beta, g, s0, reverse):
    if reverse:
        q, k, v, beta, g = (jnp.flip(a, axis=1) for a in (q, k, v, beta, g))
    o, s = _chunk_gated_delta(q, k, v, beta, g, s0)
    if reverse:
        o = jnp.flip(o, axis=1)
    return o, s


def _bounds(n, win):
    i = jnp.arange(n)
    s = jnp.clip(i - win // 2, 0, n)
    e = jnp.clip(i - win // 2 + win, 0, n)
    return s, e


def _pool_grid(u, win):
    rows = u.shape[1]
    sat = jnp.pad(jnp.cumsum(jnp.cumsum(u, axis=1), axis=2), ((0, 0), (1, 0), (1, 0), (0, 0)))
    sr, er = _bounds(rows, win)
    sc, ec = _bounds(GRID_W, win)
    top, bot = sat[:, sr], sat[:, er]
    box = bot[:, :, ec] - bot[:, :, sc] - top[:, :, ec] + top[:, :, sc]
    cnt = ((er - sr)[:, None] * (ec - sc)[None, :]).astype(u.dtype)
    return box / cnt[None, :, :, None] - u


def _pool_seq(u, win):
    t = u.shape[1]
    cs = jnp.pad(jnp.cumsum(u, axis=1), ((0, 0), (1, 0), (0, 0)))
    s, e = _bounds(t, win)
    return (cs[:, e] - cs[:, s]) / (e - s).astype(u.dtype)[None, :, None] - u


def _pool_branch(u, w_b, pool_scale, grid):
    b, t, _ = u.shape
    uf = u.astype(jnp.float32).reshape(b, t, POOL_GROUPS, POOL_GROUP_DIM)
    outs = []
    for gi, win in enumerate(POOL_WINDOWS):
        ug = uf[:, :, gi]
        if grid:
            rows = t // GRID_W
            pooled = _pool_grid(ug.reshape(b, rows, GRID_W, POOL_GROUP_DIM), win).reshape(b, t, POOL_GROUP_DIM)
        else:
            pooled = _pool_seq(ug, win)
        outs.append(pooled)
    p = jnp.stack(outs, axis=2).astype(u.dtype)
    y = jnp.einsum('btgc,gcd->btgd', p, w_b).reshape(b, t, D_MODEL)
    return y * pool_scale


def _fourier_branch(u, w_c):
    b, t, _ = u.shape
    uf = u.astype(jnp.float32).reshape(b, t, FOURIER_GROUPS, FOURIER_GROUP_DIM)
    f = jnp.fft.fftn(uf, axes=(1, 3), norm='ortho').real
    return f.astype(u.dtype).reshape(b, t, FOURIER_WIDTH) @ w_c


def _merge(o, z, pool_in, four_in, gates, o_gain, w_a, w_b, pool_scale, w_c, w_out, grid):
    b, t = z.shape[:2]
    of = o * lax.rsqrt(jnp.mean(o * o, axis=-1, keepdims=True) + RMS_EPS) * o_gain.astype(jnp.float32)
    of = of * jax.nn.silu(z.astype(jnp.float32).reshape(b, t, DN_HEADS, DN_HEAD_DIM))
    y_a = of.reshape(b, t, DN_WIDTH).astype(z.dtype) @ w_a
    y_b = _pool_branch(pool_in, w_b, pool_scale, grid)
    y_c = _fourier_branch(four_in, w_c)
    g_a, g_b, g_c = jnp.split(jax.nn.sigmoid(gates), N_BRANCH, axis=-1)
    return (g_a * y_a + g_b * y_b + g_c * y_c) @ w_out


def _mixer(hx, hc, w_in, conv_w, a_log, dt_bias, o_gain, w_a, w_b, pool_scale, w_c, w_out, need_ctx):
    lq, lk, lv, lbeta, lalpha, lz, lpool, lfour, lgates = _project(hx, w_in, conv_w)
    cq, ck, cv, cbeta, calpha, cz, cpool, cfour, cgates = _project(hc, w_in, conv_w)
    o_lat, o_ctx = 0.0, 0.0
    for d in range(2):
        rev = d == 1
        beta_c, g_c = _decay(cbeta, calpha, a_log[d], dt_bias[d], d)
        s0 = jnp.zeros((hc.shape[0], DN_HEADS, DN_HEAD_DIM, DN_HEAD_DIM), jnp.float32)
        oc, s_ctx = _scan_dir(cq, ck, cv, beta_c, g_c, s0, rev)
        beta_l, g_l = _decay(lbeta, lalpha, a_log[d], dt_bias[d], d)
        ol, _ = _scan_dir(lq, lk, lv, beta_l, g_l, s_ctx, rev)
        o_lat = o_lat + ol
        o_ctx = o_ctx + oc
    y_lat = _merge(o_lat, lz, lpool, lfour, lgates, o_gain, w_a, w_b, pool_scale, w_c, w_out, True)
    y_ctx = _merge(o_ctx, cz, cpool, cfour, cgates, o_gain, w_a, w_b, pool_scale, w_c, w_out, False) if need_ctx else None
    return y_lat, y_ctx


def setup_inputs(seed: int = 0) -> dict:
    key = jax.random.key(seed)
    ks = jax.random.split(key, 24)
    f32 = jnp.float32

    def nrm(k, shape, scale):
        return jax.random.normal(k, shape, f32) * scale

    dt = jnp.exp(jax.random.uniform(ks[13], (DEPTH, 2, DN_HEADS), f32, math.log(1e-3), math.log(0.1)))
    return {
        'x': nrm(ks[0], (BATCH, SEQ, D_MODEL), 1.0),
        'c': nrm(ks[1], (BATCH, D_MODEL), 1.0),
        'ctx': nrm(ks[2], (BATCH, CTX_LEN, D_MODEL), 1.0),
        'c_ctx': nrm(ks[3], (D_MODEL,), 1.0),
        'w_ada': nrm(ks[4], (DEPTH, D_MODEL, N_MOD * D_MODEL), 0.5 * D_MODEL ** -0.5),
        'b_ada': nrm(ks[5], (DEPTH, N_MOD * D_MODEL), 0.02),
        'norm_gain': 1.0 + nrm(ks[6], (DEPTH, 3, D_MODEL), 0.02),
        'w_ffn1_gu': nrm(ks[7], (DEPTH, D_MODEL, 2 * D_FF), D_MODEL ** -0.5),
        'w_ffn1_down': nrm(ks[8], (DEPTH, D_FF, D_MODEL), D_FF ** -0.5),
        'w_ffn2_gu': nrm(ks[9], (DEPTH, D_MODEL, 2 * D_FF), D_MODEL ** -0.5),
        'w_ffn2_down': nrm(ks[10], (DEPTH, D_FF, D_MODEL), D_FF ** -0.5),
        'w_in': nrm(ks[11], (DEPTH, D_MODEL, D_PROJ), D_MODEL ** -0.5),
        'conv_w': nrm(ks[12], (DEPTH, CONV_W, 3 * DN_WIDTH), CONV_W ** -0.5),
        'a_log': jnp.log(jax.random.uniform(ks[14], (DEPTH, 2, DN_HEADS), f32, 1.0, 16.0)),
        'dt_bias': dt + jnp.log(-jnp.expm1(-dt)),
        'o_gain': 1.0 + nrm(ks[15], (DEPTH, DN_HEAD_DIM), 0.02),
        'w_a': nrm(ks[16], (DEPTH, DN_WIDTH, D_MODEL), DN_WIDTH ** -0.5),
        'w_b': nrm(ks[17], (DEPTH, POOL_GROUPS, POOL_GROUP_DIM, D_MODEL // POOL_GROUPS), POOL_GROUP_DIM ** -0.5),
        'pool_scale': 1.0 + nrm(ks[18], (DEPTH, D_MODEL), 0.1),
        'w_c': nrm(ks[19], (DEPTH, FOURIER_WIDTH, D_MODEL), FOURIER_WIDTH ** -0.5),
        'w_out': nrm(ks[20], (DEPTH, D_MODEL, D_MODEL), D_MODEL ** -0.5),
        'final_gain': 1.0 + nrm(ks[21], (D_MODEL,), 0.02),
    }


def reference(x, c, ctx, c_ctx, w_ada, b_ada, norm_gain, w_ffn1_gu, w_ffn1_down, w_ffn2_gu, w_ffn2_down,
              w_in, conv_w, a_log, dt_bias, o_gain, w_a, w_b, pool_scale, w_c, w_out, final_gain):
    xc = ctx
    for l in range(DEPTH):
        last = l == DEPTH - 1
        m_lat = [m[:, None, :] for m in jnp.split(jax.nn.silu(c) @ w_ada[l] + b_ada[l], N_MOD, axis=-1)]
        m_ctx = jnp.split(jax.nn.silu(c_ctx) @ w_ada[l] + b_ada[l], N_MOD, axis=-1)
        x = _half_ffn(x, norm_gain[l, 0], m_lat[0], m_lat[1], m_lat[2], w_ffn1_gu[l], w_ffn1_down[l])
        xc = _half_ffn(xc, norm_gain[l, 0], m_ctx[0], m_ctx[1], m_ctx[2], w_ffn1_gu[l], w_ffn1_down[l])
        hx = _rms(x, norm_gain[l, 1]) * (1 + m_lat[4]) + m_lat[3]
        hc = _rms(xc, norm_gain[l, 1]) * (1 + m_ctx[4]) + m_ctx[3]
        y, yc = _mixer(hx, hc, w_in[l], conv_w[l], a_log[l], dt_bias[l], o_gain[l], w_a[l], w_b[l],
                       pool_scale[l], w_c[l], w_out[l], not last)
        x = x + m_lat[5] * y
        x = _half_ffn(x, norm_gain[l, 2], m_lat[6], m_lat[7], m_lat[8], w_ffn2_gu[l], w_ffn2_down[l])
        if not last:
            xc = xc + m_ctx[5] * yc
            xc = _half_ffn(xc, norm_gain[l, 2], m_ctx[6], m_ctx[7], m_ctx[8], w_ffn2_gu[l], w_ffn2_down[l])
    return _rms(x, final_gain)
```

```python
import math
from contextlib import ExitStack

import numpy as np
import concourse.bass as bass
import concourse.mybir as mybir
from concourse.bass_utils import run_bass_kernel_spmd

F32 = mybir.dt.float32
BF16 = mybir.dt.bfloat16
ALU = mybir.AluOpType
AF = mybir.ActivationFunctionType

D = 1024
KC = 8
DFF = 2816
NJ = 22
DPROJ = 5648
NT = 512
CH = 64


import types


def _freeze(fn):
    if fn.__closure__ is None:
        return fn
    cells = []
    for c in fn.__closure__:
        try:
            cells.append(types.CellType(c.cell_contents))
        except ValueError:
            cells.append(c)
    return types.FunctionType(fn.__code__, fn.__globals__, fn.__name__, fn.__defaults__, tuple(cells))


class Buf:
    __slots__ = ("ap", "w", "r", "name", "psum")

    def __init__(self, ap, name="", psum=False):
        self.ap = ap
        self.w = {}
        self.r = {}
        self.name = name
        self.psum = psum


class Sched:
    ENG = ("pe", "act", "dve", "pool", "sp")

    def __init__(self, nc, n_dma=16, epoch=24000, same_eng_sync=True):
        self.nc = nc
        self.q = {e: [] for e in self.ENG}
        self.sems = []
        self.epoch = epoch
        self.same = same_eng_sync
        self.esem = {}
        self.ecnt = {}
        for e in self.ENG:
            self.esem[e] = self._new_sem("e_" + e)
            self.ecnt[e] = 0
        self.known = {e: {} for e in self.ENG}
        self.slots = [[self._new_sem("d%d" % i), 0] for i in range(n_dma)]
        self.slot_i = 0
        self.guard_eng = "act"
        self.guard_fn = None
        self.dhist = {}
        import os
        self.maxout = int(os.environ.get("MAXOUT", "4"))
        self.nwait = 0
        self.nins = 0

    def _new_sem(self, name):
        h = self.nc.alloc_semaphore("%s_%d" % (name, len(self.sems)))
        self.sems.append(h)
        return len(self.sems) - 1

    def _wait(self, eng, evs):
        kn = self.known[eng]
        for s, v in evs.items():
            if kn.get(s, 0) < v:
                kn[s] = v
                h = self.sems[s]
                self.q[eng].append(lambda e, h=h, v=v: e.wait_ge(h, v))
                self.nwait += 1

    def _deps(self, eng, reads, writes):
        evs = {}

        def add(d):
            for s, v in d.items():
                if evs.get(s, 0) < v:
                    evs[s] = v

        for b in reads:
            add(b.w)
        for b in writes:
            add(b.w)
            add(b.r)
        if eng == "pe" or not self.same:
            evs.pop(self.esem[eng], None)
        return evs

    def _mark(self, ev, reads, writes):
        s, v = ev
        for b in reads:
            if b.r.get(s, 0) < v:
                b.r[s] = v
        for b in writes:
            b.w = {s: v}
            b.r = {}

    def op(self, eng, fn, reads=(), writes=()):
        fn = _freeze(fn)
        evs = self._deps(eng, reads, writes)
        self._wait(eng, evs)
        if self.ecnt[eng] >= self.epoch:
            self.esem[eng] = self._new_sem("e_" + eng)
            self.ecnt[eng] = 0
        self.ecnt[eng] += 1
        s = self.esem[eng]
        h = self.sems[s]
        self.q[eng].append(lambda e, fn=fn, h=h: fn(e).then_inc(h, 1))
        self.nins += 1
        if eng == self.guard_eng and self.guard_fn is not None and any(b.psum for b in reads):
            self._mark((s, self.ecnt[eng]), [], writes)
            self.ecnt[eng] += 1
            gf = self.guard_fn
            self.q[eng].append(lambda e, gf=gf, h=h: gf(e).then_inc(h, 1))
            self.nins += 1
            self._mark((s, self.ecnt[eng]), reads, [])
        else:
            self._mark((s, self.ecnt[eng]), reads, writes)

    def dma(self, eng, out_ap, in_ap, reads=(), writes=(), **kw):
        fn_dummy = None
        evs = self._deps(eng, reads, writes)
        if eng == "pool":
            slot = [self._new_sem("sw"), 0]
        else:
            slot = self.slots[self.slot_i]
            self.slot_i = (self.slot_i + 1) % len(self.slots)
        if slot[1] > 0 and evs.get(slot[0], 0) < slot[1]:
            evs[slot[0]] = slot[1]
        hist = self.dhist.setdefault(eng, [])
        mo = 1 if eng == "pool" else self.maxout
        if len(hist) >= mo:
            s_, v_ = hist[-mo]
            if evs.get(s_, 0) < v_:
                evs[s_] = v_
        self._wait(eng, evs)
        slot[1] += 16
        hist.append((slot[0], slot[1]))
        if len(hist) > 64:
            del hist[:32]
        h = self.sems[slot[0]]
        self.q[eng].append(
            lambda e, h=h, o=out_ap, i=in_ap, kw=kw: e.dma_start(out=o, in_=i, **kw).then_inc(h, 16)
        )
        self.nins += 1
        self._mark((slot[0], slot[1]), reads, writes)

    def barrier(self):
        evs = {}
        for e in self.ENG:
            if self.ecnt[e] > 0:
                evs[self.esem[e]] = self.ecnt[e]
        for s, v in self.slots:
            if v > 0:
                evs[s] = v
        for e in self.ENG:
            d = dict(evs)
            if e == "pe":
                d.pop(self.esem[e], None)
            self._wait(e, d)

    def emit(self):
        q = self.q
        with self.nc.Block() as block:

            @block.tensor
            def _(e):
                for f in q["pe"]:
                    f(e)

            @block.scalar
            def _(e):
                for f in q["act"]:
                    f(e)

            @block.vector
            def _(e):
                for f in q["dve"]:
                    f(e)

            @block.gpsimd
            def _(e):
                for f in q["pool"]:
                    f(e)

            @block.sync
            def _(e):
                for f in q["sp"]:
                    f(e)


def _bounds(n, win):
    i = np.arange(n)
    s = np.clip(i - win // 2, 0, n)
    e = np.clip(i - win // 2 + win, 0, n)
    return s, e


def _pool_mat_1d(n, win, normalize=True):
    s, e = _bounds(n, win)
    P = np.zeros((n, n), np.float64)
    for i in range(n):
        P[i, s[i]:e[i]] = 1.0 / (e[i] - s[i])
    return P


def pool_blocks(TL, TC):
    wins = (2, 4, 8, 16)
    blocks = []
    cache = {}
    plan = {}

    def add_block(b):
        key = b.tobytes()
        if key not in cache:
            cache[key] = len(blocks)
            blocks.append(b)
        return cache[key]

    rows = TL // 64
    for g, win in enumerate(wins):
        Pr = _pool_mat_1d(rows, win)
        Pc = _pool_mat_1d(64, win)
        for tile in range(TL // NT):
            r0 = tile * 8
            lst = []
            for src in range(TL // 128):
                sr0 = src * 2
                blk = np.zeros((128, NT), np.float32)
                nz = False
                for dr in range(2):
                    for orow in range(8):
                        wgt = Pr[r0 + orow, sr0 + dr]
                        if wgt != 0.0:
                            blk[dr * 64:(dr + 1) * 64, orow * 64:(orow + 1) * 64] += (wgt * Pc.T).astype(np.float32)
                            nz = True
                        if sr0 + dr == r0 + orow:
                            nz = True
                            blk[dr * 64:(dr + 1) * 64, orow * 64:(orow + 1) * 64] -= np.eye(64, dtype=np.float32)
                if nz:
                    lst.append((src * 128, add_block(blk), NT))
            plan[(0, tile, g)] = lst
        P1 = _pool_mat_1d(TC, win) - np.eye(TC)
        lst = []
        for src in range(TC // 128):
            blk = np.zeros((128, NT), np.float32)
            blk[:, :TC] = P1[:, src * 128:(src + 1) * 128].T
            lst.append((TL + src * 128, add_block(blk), TC))
        plan[(1, 0, g)] = lst
    return np.stack(blocks), plan


def host_consts(TL, TC):
    c = {}
    c["ident"] = np.eye(128, dtype=np.float32)
    i = np.arange(CH)
    incl = np.stack([(i[None, :] >= i[:, None]), (i[None, :] <= i[:, None])]).astype(np.float32)
    strict = np.stack([(i[None, :] > i[:, None]), (i[None, :] < i[:, None])]).astype(np.float32)
    c["lt"] = incl.copy()
    c["negm"] = ((1.0 - incl) * -30000.0).astype(np.float32)
    lv = np.zeros((2, 6, CH, CH), np.float32)
    for li, b in enumerate((1, 2, 4, 8, 16, 32)):
        bi = i // b
        m = ((bi[None, :] % 2 == 1) & (bi[:, None] == bi[None, :] - 1)).astype(np.float32)
        lv[0, li] = m * strict[0]
        lv[1, li] = m.T * strict[1]
    c["lvm"] = lv
    kk = np.arange(64)
    ang = 2 * np.pi * np.outer(kk, kk) / 64.0
    C64 = np.cos(ang) / 8.0
    S64 = np.sin(ang) / 8.0
    cs = np.zeros((128, 256), np.float32)
    for b in range(2):
        cs[b * 64:(b + 1) * 64, b * 64:(b + 1) * 64] = C64
        cs[b * 64:(b + 1) * 64, 128 + b * 64:128 + (b + 1) * 64] = S64
    c["cs"] = cs
    for nm, T in (("dftl", TL), ("dftc", TC)):
        t = np.arange(T)
        a = 2 * np.pi * ((np.outer(t, t)) % T) / T
        sc = 1.0 / math.sqrt(T)
        c[nm] = np.stack([np.cos(a) * sc, -np.sin(a) * sc]).astype(np.float32)
    pb, plan = pool_blocks(TL, TC)
    c["pblk"] = pb
    return c, plan


def build_program(TL=4096, TC=256, DEPTH=4, debug=False, plan=None, npblk=1, stages=99, upto=99):
    nc = bass.Bass("TRN2", target_bir_lowering=False)
    TT = TL + TC
    skind = "ExternalOutput" if debug else "Internal"

    def din(name, shape, dt=F32):
        return nc.dram_tensor(name, list(shape), dt, kind="ExternalInput").ap()

    def dscr(name, shape, dt=F32):
        return nc.dram_tensor(name, list(shape), dt, kind=skind).ap()

    x_in = din("x", [TL, D])
    ctx_in = din("ctx", [TC, D])
    cT_in = din("cT", [128, KC, 2])
    w_ada = din("w_ada", [DEPTH, D, 9 * D])
    b_adaT = din("b_adaT", [DEPTH, 128, 72])
    gainT = din("gainT", [DEPTH, 128, 3, KC])
    fgainT = din("fgainT", [128, KC])
    w1gu = din("w_ffn1_gu", [DEPTH, D, 2 * DFF])
    w1d = din("w_ffn1_down", [DEPTH, DFF, D])
    w2gu = din("w_ffn2_gu", [DEPTH, D, 2 * DFF])
    w2d = din("w_ffn2_down", [DEPTH, DFF, D])
    w_in = din("w_in", [DEPTH, D, DPROJ])
    convT = din("convT", [DEPTH, 128, 12, 3])
    nexpA = din("alogB", [DEPTH, 128, 8])
    dtbB = din("dtbB", [DEPTH, 128, 8])
    ogT = din("ogT", [DEPTH, 128, 1])
    w_a = din("w_a", [DEPTH, 512, D])
    w_b = din("w_b", [DEPTH, 4, 64, 256])
    pscT = din("pscT", [DEPTH, 128, KC])
    w_c = din("w_c", [DEPTH, 256, D])
    w_out = din("w_out", [DEPTH, D, D])
    ident_in = din("ident", [128, 128])
    lt_in = din("lt", [2, CH, CH])
    negm_in = din("negm", [2, CH, CH])
    lvm_in = din("lvm", [2, 6, CH, CH])
    cs_in = din("cs", [128, 256])
    dftl_in = din("dftl", [2, TL, TL])
    dftc_in = din("dftc", [2, TC, TC])
    pblk_in = din("pblk", [npblk, 128, NT])
    out = nc.dram_tensor("out", [TL, D], F32, kind="ExternalOutput").ap()

    xs_d = dscr("xs_d", [128, KC, TT])
    hx_d = dscr("hx_d", [128, KC, TT], BF16)
    qkv_d = dscr("qkv_d", [128, 12, TT])
    zs_d = dscr("zs_d", [128, 4, TT])
    ab_d = dscr("ab_d", [TT, 512], BF16)
    pool_d = dscr("pool_d", [TT, 256], BF16)
    bg_d = dscr("bg_d", [TT, 16])
    qk_d = dscr("qk_d", [128, 8, TT], BF16)
    kv_d = dscr("kv_d", [TT, 1024], BF16)
    o_d = dscr("o_d", [2, 128, 4, TT])
    f_d = dscr("f_d", [128, 2, TT], BF16)
    dftl_b = dscr("dftl_b", [2, TL, TL], BF16)
    dftc_b = dscr("dftc_b", [2, TC, TC], BF16)
    pblk_b = dscr("pblk_b", [npblk, 128, NT], BF16)

    S = Sched(nc)
    gstack = ExitStack()

    WB = {}

    def wscr(name, shape):
        return Buf(nc.dram_tensor(name, list(shape), BF16, kind="Internal").ap(), name)

    for l_ in range(DEPTH):
        WB[("ada", l_)] = wscr("wb_ada%d" % l_, [D, 9 * D])
        WB[("g1", l_)] = wscr("wb_g1%d" % l_, [D, 2 * DFF])
        WB[("d1", l_)] = wscr("wb_d1%d" % l_, [DFF, D])
        WB[("g2", l_)] = wscr("wb_g2%d" % l_, [D, 2 * DFF])
        WB[("d2", l_)] = wscr("wb_d2%d" % l_, [DFF, D])
        WB[("in", l_)] = wscr("wb_in%d" % l_, [D, DPROJ])
        WB[("a", l_)] = wscr("wb_a%d" % l_, [512, D])
        WB[("b", l_)] = wscr("wb_b%d" % l_, [4, 64, 256])
        WB[("c", l_)] = wscr("wb_c%d" % l_, [256, D])
        WB[("out", l_)] = wscr("wb_out%d" % l_, [D, D])
    WB["cs"] = wscr("wb_cs", [128, 256])
    WB["dftl"] = Buf(dftl_b, "dftl_b")
    WB["dftc"] = Buf(dftc_b, "dftc_b")

    def flat2(ap):
        nd = len(ap.shape)
        names = " ".join("d%d" % i for i in range(nd))
        f = ap.rearrange("%s -> (%s)" % (names, names)) if nd > 1 else ap
        return f.rearrange("(r c) -> r c", c=2048)

    def conv(key, src_ap):
        b = WB[key]
        S.dma("pool", flat2(b.ap), flat2(src_ap), writes=[b])

    def conv_layer(l_):
        conv(("ada", l_), w_ada[l_])
        conv(("g1", l_), w1gu[l_])
        conv(("d1", l_), w1d[l_])
        conv(("in", l_), w_in[l_])
        if l_ == 0:
            conv("cs", cs_in)
            conv("dftc", dftc_in)
            for cs_ in range(2):
                S.dma("pool", flat2(dftl_b[cs_]), flat2(dftl_in[cs_]), writes=[WB["dftl"]])
            WB["pblk"] = Buf(pblk_b, "pblk_b")
            S.dma("pool", flat2(pblk_b), flat2(pblk_in), writes=[WB["pblk"]])
        conv(("a", l_), w_a[l_])
        conv(("b", l_), w_b[l_])
        conv(("c", l_), w_c[l_])
        conv(("out", l_), w_out[l_])
        conv(("g2", l_), w2gu[l_])
        conv(("d2", l_), w2d[l_])

    uniq = [0]

    def sb(stack, name, shape, dt=F32):
        uniq[0] += 1
        nm = "s%d_%s" % (uniq[0], name)
        return Buf(stack.enter_context(nc.sbuf_tensor(nm, list(shape), dt)).ap(), nm)

    PS = [Buf(gstack.enter_context(nc.psum_tensor("ps%d" % i, [128, NT], F32)).ap(), "ps%d" % i, psum=True) for i in range(8)]
    psi = [0]

    def ps():
        b = PS[psi[0] % 8]
        psi[0] += 1
        return b

    ident = sb(gstack, "ident", [128, 128])
    identb = sb(gstack, "identb", [128, 128], BF16)
    onesb = sb(gstack, "onesb", [128, 128], BF16)
    onesf = sb(gstack, "onesf", [128, 128])
    mods = sb(gstack, "mods", [128, DEPTH, 9, KC, 2])
    Amod = sb(gstack, "Amod", [128, DEPTH, 3, KC, 2])
    Gmod = sb(gstack, "Gmod", [128, DEPTH, 3, KC, 2])
    gain_sb = sb(gstack, "gain_sb", [128, DEPTH, 3, KC])
    fgain_sb = sb(gstack, "fgain_sb", [128, KC])
    zero8 = sb(gstack, "zero8", [128, KC])
    dumA = sb(gstack, "dumA", [128, 1])
    dumB = sb(gstack, "dumB", [128, 1])
    S.op("dve", lambda e: e.memset(dumB.ap, 0.0), writes=[dumB])
    import os
    if os.environ.get("NOGUARD", "") != "1":
        S.guard_fn = lambda e: e.activation(out=dumA.ap, in_=dumB.ap, func=AF.Identity)

    S.dma("sp", ident.ap, ident_in, writes=[ident])
    S.dma("sp", gain_sb.ap, gainT.rearrange("l p i k -> p l i k"), writes=[gain_sb])
    S.dma("sp", fgain_sb.ap, fgainT, writes=[fgain_sb])
    S.op("dve", lambda e: e.tensor_copy(out=identb.ap, in_=ident.ap), reads=[ident], writes=[identb])
    S.op("dve", lambda e: e.memset(onesb.ap, 1.0), writes=[onesb])
    S.op("dve", lambda e: e.memset(onesf.ap, 1.0), writes=[onesf])
    S.op("dve", lambda e: e.memset(zero8.ap, 0.0), writes=[zero8])

    tiles = [(i * NT, NT, 0) for i in range(TL // NT)] + [(TL, TC, 1)]

    scb = sb(gstack, "scb", [128, KC, 2], BF16)
    badd = sb(gstack, "badd", [128, DEPTH, 72])
    ct_ = sb(gstack, "ct", [128, KC, 2])
    S.dma("sp", ct_.ap, cT_in, writes=[ct_])
    S.dma("sp", badd.ap, b_adaT.rearrange("l p j -> p l j"), writes=[badd])
    S.op("act", lambda e: e.activation(out=scb.ap, in_=ct_.ap, func=AF.Silu), reads=[ct_], writes=[scb])

    def phase_mods(l):
        with ExitStack() as st:
            wa = [sb(st, "wada%d" % i, [128, KC, D], BF16) for i in range(2)]
            wsrc = WB[("ada", l)]
            for m in range(9):
                w = wa[m % 2]
                for hh in range(2):
                    S.dma("sp", w.ap[:, :, hh * 512:(hh + 1) * 512],
                          wsrc.ap[:, m * D + hh * 512:m * D + (hh + 1) * 512].rearrange("(k p) c -> p k c", p=128),
                          reads=[wsrc], writes=[w])
                p = ps()
                for j in range(KC):
                    for k in range(KC):
                        S.op("pe", lambda e: e.matmul(
                            p.ap[:, j * 2:(j + 1) * 2], lhsT=w.ap[:, k, j * 128:(j + 1) * 128], rhs=scb.ap[:, k, :],
                            start=(k == 0), stop=(k == KC - 1)), reads=[w, scb], writes=[p])
                S.op("dve", lambda e: e.tensor_tensor(
                    out=mods.ap[:, l, m], in0=p.ap[:, 0:16].rearrange("p (k s) -> p k s", s=2),
                    in1=badd.ap[:, l, m * 8:(m + 1) * 8].unsqueeze(2).to_broadcast([128, KC, 2]), op=ALU.add),
                    reads=[p, badd], writes=[mods])
            for i3 in range(3):
                S.op("dve", lambda e: e.tensor_scalar(
                    out=Amod.ap[:, l, i3], in0=mods.ap[:, l, 3 * i3 + 1], scalar1=1.0, scalar2=None, op0=ALU.add),
                    reads=[mods], writes=[Amod])
                S.op("dve", lambda e: e.tensor_tensor(
                    out=Amod.ap[:, l, i3], in0=Amod.ap[:, l, i3],
                    in1=gain_sb.ap[:, l, i3].unsqueeze(2).to_broadcast([128, KC, 2]), op=ALU.mult),
                    reads=[Amod, gain_sb], writes=[Amod])
                S.op("dve", lambda e: e.tensor_scalar(
                    out=Gmod.ap[:, l, i3], in0=mods.ap[:, l, 3 * i3 + 2], scalar1=(1.0 if i3 == 1 else 0.5),
                    scalar2=None, op0=ALU.mult), reads=[mods], writes=[Gmod])
            S.barrier()

    def phase_xin():
        with ExitStack() as st:
            xin = [sb(st, "xin%d" % i, [128, 4, D]) for i in range(2)]
            xo = [sb(st, "xo%d" % i, [128, KC, NT]) for i in range(2)]
            for ti, (t0, n, seq) in enumerate(tiles):
                nb = n // 128
                a = xin[ti % 2]
                o = xo[ti % 2]
                src = x_in[t0:t0 + n] if seq == 0 else ctx_in[0:n]
                S.dma("sp", a.ap[:, :nb, :], src.rearrange("(b p) f -> p b f", p=128), writes=[a])
                for k in range(KC):
                    p = ps()
                    for b in range(nb):
                        S.op("pe", lambda e, p=p, a=a, b=b, k=k: e.transpose(
                            p.ap[:, b * 128:(b + 1) * 128], a.ap[:, b, k * 128:(k + 1) * 128], ident.ap),
                            reads=[a, ident], writes=[p])
                    eng = "act" if k % 2 else "dve"
                    if eng == "act":
                        S.op("act", lambda e, p=p, o=o, k=k, n=n: e.activation(out=o.ap[:, k, :n], in_=p.ap[:, :n], func=AF.Identity),
                             reads=[p], writes=[o])
                    else:
                        S.op("dve", lambda e, p=p, o=o, k=k, n=n: e.tensor_copy(out=o.ap[:, k, :n], in_=p.ap[:, :n]),
                             reads=[p], writes=[o])
                S.dma("sp", xs_d[:, :, t0:t0 + n], o.ap[:, :, :n], reads=[o])
            S.barrier()

    def rmsmod(st_bufs, xt, n, a_ap, b_ap, hout, out_dt_is_bf16=True):
        sq, r1, rstd, xn = st_bufs
        S.op("act", lambda e: e.activation(out=sq.ap[:, :, :n], in_=xt.ap[:, :, :n], func=AF.Square), reads=[xt], writes=[sq])
        p = ps()
        for k in range(KC):
            S.op("pe", lambda e, k=k, p=p: e.matmul(p.ap[:, :n], lhsT=onesb.ap, rhs=sq.ap[:, k, :n], start=(k == 0), stop=(k == KC - 1)),
                 reads=[sq, onesb], writes=[p])
        S.op("act", lambda e, p=p: e.activation(out=r1.ap[:, :n], in_=p.ap[:, :n], func=AF.Sqrt, scale=1.0 / D, bias=1e-6),
             reads=[p], writes=[r1])
        S.op("dve", lambda e: e.reciprocal(out=rstd.ap[:, :n], in_=r1.ap[:, :n]), reads=[r1], writes=[rstd])
        S.op("dve", lambda e: e.tensor_tensor(out=xn.ap[:, :, :n], in0=xt.ap[:, :, :n],
                                               in1=rstd.ap[:, :n].unsqueeze(1).to_broadcast([128, KC, n]), op=ALU.mult),
             reads=[xt, rstd], writes=[xn])
        for k in range(KC):
            av = a_ap(k)
            bv = b_ap(k) if b_ap else 0.0
            if k % 2 == 0:
                S.op("act", lambda e, k=k: e.activation(out=hout.ap[:, k, :n], in_=xn.ap[:, k, :n], func=AF.Identity,
                                                         scale=av, bias=bv),
                     reads=[xn, Amod, mods, fgain_sb], writes=[hout])
            else:
                S.op("dve", lambda e, k=k: e.tensor_scalar(out=hout.ap[:, k, :n], in0=xn.ap[:, k, :n], scalar1=av,
                                                            scalar2=bv, op0=ALU.mult, op1=ALU.add),
                     reads=[xn, Amod, mods, fgain_sb], writes=[hout])

    def norm_bufs(st, tag=""):
        return (sb(st, "sq" + tag, [128, KC, NT], BF16), sb(st, "r1" + tag, [128, NT]), sb(st, "rstd" + tag, [128, NT]),
                sb(st, "xn" + tag, [128, KC, NT]))

    def phase_ffn(l, i3, wgu_d, wd_d, do_ctx=True):
        with ExitStack() as st:
            xb = [sb(st, "fx%d" % i, [128, KC, NT]) for i in range(2)]
            nb_ = norm_bufs(st)
            h = sb(st, "fh", [128, KC, NT], BF16)
            g = sb(st, "fg", [128, NJ, NT], BF16)
            wg = [sb(st, "fwg%d" % i, [128, KC, 512], BF16) for i in range(3)]
            wd = [sb(st, "fwd%d" % i, [128, NJ, 512], BF16) for i in range(2)]
            sa = [sb(st, "fsa%d" % i, [128, NT]) for i in range(2)]
            wgi = 0
            wdi = 0
            for ti, (t0, n, seq) in enumerate(tiles):
                if seq == 1 and not do_ctx:
                    continue
                xt = xb[ti % 2]
                S.dma("sp", xt.ap[:, :, :n], xs_d[:, :, t0:t0 + n], writes=[xt])
                rmsmod(nb_, xt, n, lambda k: Amod.ap[:, l, i3, k, seq:seq + 1], lambda k: mods.ap[:, l, 3 * i3, k, seq:seq + 1], h)
                for jb in range(NJ // 2):
                    w = wg[wgi % 3]
                    wgi += 1
                    S.dma("sp", w.ap[:, :, 0:256], wgu_d.ap[:, jb * 256:(jb + 1) * 256].rearrange("(k p) c -> p k c", p=128), reads=[wgu_d], writes=[w])
                    S.dma("sp", w.ap[:, :, 256:512], wgu_d.ap[:, DFF + jb * 256:DFF + (jb + 1) * 256].rearrange("(k p) c -> p k c", p=128), reads=[wgu_d], writes=[w])
                    for jj in range(2):
                        j = 2 * jb + jj
                        pa = ps()
                        pu = ps()
                        for k in range(KC):
                            S.op("pe", lambda e, pa=pa, w=w, k=k, jj=jj: e.matmul(
                                pa.ap[:, :n], lhsT=w.ap[:, k, jj * 128:(jj + 1) * 128], rhs=h.ap[:, k, :n], start=(k == 0), stop=(k == KC - 1)),
                                reads=[w, h], writes=[pa])
                        for k in range(KC):
                            S.op("pe", lambda e, pu=pu, w=w, k=k, jj=jj: e.matmul(
                                pu.ap[:, :n], lhsT=w.ap[:, k, 256 + jj * 128:256 + (jj + 1) * 128], rhs=h.ap[:, k, :n], start=(k == 0), stop=(k == KC - 1)),
                                reads=[w, h], writes=[pu])
                        s_ = sa[j % 2]
                        S.op("act", lambda e, pa=pa, s_=s_: e.activation(out=s_.ap[:, :n], in_=pa.ap[:, :n], func=AF.Silu), reads=[pa], writes=[s_])
                        S.op("dve", lambda e, pu=pu, s_=s_, j=j: e.tensor_tensor(out=g.ap[:, j, :n], in0=pu.ap[:, :n], in1=s_.ap[:, :n], op=ALU.mult),
                             reads=[pu, s_], writes=[g])
                for half in range(2):
                    w = wd[wdi % 2]
                    wdi += 1
                    for q4 in range(2):
                        S.dma("sp", w.ap[:, q4 * 11:(q4 + 1) * 11, :],
                              wd_d.ap[q4 * 11 * 128:(q4 + 1) * 11 * 128, half * 512:(half + 1) * 512].rearrange("(j p) c -> p j c", p=128), reads=[wd_d], writes=[w])
                    for mm in range(4):
                        m = half * 4 + mm
                        py = ps()
                        for j in range(NJ):
                            S.op("pe", lambda e, py=py, w=w, j=j, mm=mm: e.matmul(
                                py.ap[:, :n], lhsT=w.ap[:, j, mm * 128:(mm + 1) * 128], rhs=g.ap[:, j, :n], start=(j == 0), stop=(j == NJ - 1)),
                                reads=[w, g], writes=[py])
                        S.op("dve", lambda e, py=py, m=m, xt=xt, seq=seq: e.scalar_tensor_tensor(
                            out=xt.ap[:, m, :n], in0=py.ap[:, :n], scalar=Gmod.ap[:, l, i3, m, seq:seq + 1], in1=xt.ap[:, m, :n],
                            op0=ALU.mult, op1=ALU.add), reads=[py, xt, Gmod], writes=[xt])
                S.dma("sp", xs_d[:, :, t0:t0 + n], xt.ap[:, :, :n], reads=[xt])
            S.barrier()

    def phase_final():
        with ExitStack() as st:
            xb = [sb(st, "lx%d" % i, [128, KC, NT]) for i in range(2)]
            nb_ = norm_bufs(st)
            hf = sb(st, "lh", [128, KC, NT])
            ot = [sb(st, "lo%d" % i, [128, 4, D]) for i in range(2)]
            for ti, (t0, n, seq) in enumerate(tiles):
                if seq == 1:
                    continue
                xt = xb[ti % 2]
                o = ot[ti % 2]
                S.dma("sp", xt.ap[:, :, :n], xs_d[:, :, t0:t0 + n], writes=[xt])
                rmsmod(nb_, xt, n, lambda k: fgain_sb.ap[:, k:k + 1], None, hf)
                for b in range(n // 128):
                    for kh in range(2):
                        p = ps()
                        for k4 in range(4):
                            k = kh * 4 + k4
                            S.op("pe", lambda e, p=p, b=b, k=k, k4=k4: e.transpose(
                                p.ap[:, k4 * 128:(k4 + 1) * 128], hf.ap[:, k, b * 128:(b + 1) * 128], ident.ap),
                                reads=[hf, ident], writes=[p])
                        if kh == 0:
                            S.op("act", lambda e, p=p, o=o, b=b: e.activation(out=o.ap[:, b, 0:512], in_=p.ap, func=AF.Identity), reads=[p], writes=[o])
                        else:
                            S.op("dve", lambda e, p=p, o=o, b=b: e.tensor_copy(out=o.ap[:, b, 512:1024], in_=p.ap), reads=[p], writes=[o])
                S.dma("sp", out[t0:t0 + n].rearrange("(b p) f -> p b f", p=128), o.ap[:, :n // 128, :], reads=[o])
            S.barrier()


    def phase_proj(l):
        with ExitStack() as st:
            xb = [sb(st, "px%d" % i, [128, KC, NT]) for i in range(2)]
            nb_ = norm_bufs(st)
            h = sb(st, "ph", [128, KC, NT], BF16)
            wq = [sb(st, "pw%d" % i, [128, KC, 512], BF16) for i in range(3)]
            wm = sb(st, "pwm", [128, KC, 288], BF16)
            wf = sb(st, "pwf", [128, KC, 256], BF16)
            csb = sb(st, "pcs", [128, 256], BF16)
            qst = sb(st, "pqst", [128, 16, NT])
            fT = sb(st, "pfT", [128, 2, NT], BF16)
            abst = sb(st, "pab", [128, 4, 512], BF16)
            plst = sb(st, "ppl", [128, 4, 256], BF16)
            ba = sb(st, "pba", [128, 4, 16])
            bg = sb(st, "pbg", [128, 4, 16])
            tmp8 = sb(st, "ptmp", [128, 4, 8])
            negA = sb(st, "pnegA", [128, 8])
            dtb = sb(st, "pdtb", [128, 8])
            S.dma("sp", negA.ap, nexpA[l], writes=[negA])
            S.dma("sp", dtb.ap, dtbB[l], writes=[dtb])
            S.op("act", lambda e: e.activation(out=negA.ap, in_=negA.ap, func=AF.Exp), reads=[negA], writes=[negA])
            S.op("dve", lambda e: e.tensor_scalar(out=negA.ap, in0=negA.ap, scalar1=-1.0, scalar2=None, op0=ALU.mult), reads=[negA], writes=[negA])
            win = WB[("in", l)]
            S.dma("sp", csb.ap, WB["cs"].ap, reads=[WB["cs"]], writes=[csb])
            S.dma("sp", wm.ap[:, :, 0:272], win.ap[:, 2048:2320].rearrange("(k p) c -> p k c", p=128), reads=[win], writes=[wm])
            S.dma("sp", wf.ap, win.ap[:, 2320:2576].rearrange("(k p) c -> p k c", p=128), reads=[win], writes=[wf])
            wi = 0
            for ti, (t0, n, seq) in enumerate(tiles):
                nb = n // 128
                xt = xb[ti % 2]
                S.dma("sp", xt.ap[:, :, :n], xs_d[:, :, t0:t0 + n], writes=[xt])
                rmsmod(nb_, xt, n, lambda k: Amod.ap[:, l, 1, k, seq:seq + 1], lambda k: mods.ap[:, l, 3, k, seq:seq + 1], h)
                S.dma("sp", hx_d[:, :, t0:t0 + n], h.ap[:, :, :n], reads=[h])
                if upto < 1.2:
                    continue
                for blk in range(4):
                    w = wq[wi % 3]
                    wi += 1
                    S.dma("sp", w.ap, win.ap[:, blk * 512:(blk + 1) * 512].rearrange("(k p) c -> p k c", p=128), reads=[win], writes=[w])
                    for cc in range(4):
                        j = blk * 4 + cc
                        p = ps()
                        for k in range(KC):
                            S.op("pe", lambda e: e.matmul(p.ap[:, :n], lhsT=w.ap[:, k, cc * 128:(cc + 1) * 128], rhs=h.ap[:, k, :n],
                                                          start=(k == 0), stop=(k == KC - 1)), reads=[w, h], writes=[p])
                        if blk == 3:
                            S.op("act", lambda e: e.activation(out=qst.ap[:, j, :n], in_=p.ap[:, :n], func=AF.Silu), reads=[p], writes=[qst])
                        elif cc % 2 == 0:
                            S.op("act", lambda e: e.activation(out=qst.ap[:, j, :n], in_=p.ap[:, :n], func=AF.Identity), reads=[p], writes=[qst])
                        else:
                            S.op("dve", lambda e: e.tensor_copy(out=qst.ap[:, j, :n], in_=p.ap[:, :n]), reads=[p], writes=[qst])
                for q3 in range(3):
                    S.dma("sp", qkv_d[:, q3 * 4:(q3 + 1) * 4, t0:t0 + n], qst.ap[:, q3 * 4:(q3 + 1) * 4, :n], reads=[qst])
                S.dma("sp", zs_d[:, :, t0:t0 + n], qst.ap[:, 12:16, :n], reads=[qst])
                if upto < 1.3:
                    continue
                for c in range(2):
                    p = ps()
                    for k in range(KC):
                        S.op("pe", lambda e: e.matmul(p.ap[:, :n], lhsT=wf.ap[:, k, c * 128:(c + 1) * 128], rhs=h.ap[:, k, :n],
                                                      start=(k == 0), stop=(k == KC - 1)), reads=[wf, h], writes=[p])
                    if upto != 1.35:
                        S.op("act", lambda e: e.activation(out=fT.ap[:, c, :n], in_=p.ap[:, :n], func=AF.Identity), reads=[p], writes=[fT])
                if upto < 1.4:
                    continue
                for b in range(nb):
                    if upto != 1.41:
                        p = ps()
                        for c in range(2):
                            S.op("pe", lambda e: e.matmul(p.ap[:, c * 256:(c + 1) * 256], lhsT=fT.ap[:, c, b * 128:(b + 1) * 128], rhs=csb.ap,
                                                          start=True, stop=True), reads=[fT, csb], writes=[p])
                        if os.environ.get("ABST_ACT", "") == "1":
                            S.op("act", lambda e: e.activation(out=abst.ap[:, b, :], in_=p.ap, func=AF.Identity), reads=[p], writes=[abst])
                        else:
                            S.op("dve", lambda e: e.tensor_copy(out=abst.ap[:, b, :], in_=p.ap), reads=[p], writes=[abst])
                    if upto in (1.42, 1.43, 1.44):
                        continue
                    p2 = ps()
                    for k in range(KC):
                        S.op("pe", lambda e: e.matmul(p2.ap[:, 0:272], lhsT=h.ap[:, k, b * 128:(b + 1) * 128], rhs=wm.ap[:, k, 0:272],
                                                      start=(k == 0), stop=(k == KC - 1)), reads=[wm, h], writes=[p2])
                    S.op("dve", lambda e: e.tensor_copy(out=ba.ap[:, b, :], in_=p2.ap[:, 0:16]), reads=[p2], writes=[ba])
                    S.op("dve", lambda e: e.tensor_copy(out=plst.ap[:, b, :], in_=p2.ap[:, 16:272]), reads=[p2], writes=[plst])
                if upto < 1.5:
                    continue
                S.op("act", lambda e: e.activation(out=bg.ap[:, :nb, 0:8], in_=ba.ap[:, :nb, 0:8], func=AF.Sigmoid), reads=[ba], writes=[bg])
                S.op("dve", lambda e: e.tensor_tensor(out=tmp8.ap[:, :nb, :], in0=ba.ap[:, :nb, 8:16],
                                                       in1=dtb.ap.unsqueeze(1).to_broadcast([128, nb, 8]), op=ALU.add), reads=[ba, dtb], writes=[tmp8])
                S.op("act", lambda e: e.activation(out=tmp8.ap[:, :nb, :], in_=tmp8.ap[:, :nb, :], func=AF.Exp), reads=[tmp8], writes=[tmp8])
                S.op("act", lambda e: e.activation(out=tmp8.ap[:, :nb, :], in_=tmp8.ap[:, :nb, :], func=AF.Ln, bias=1.0), reads=[tmp8], writes=[tmp8])
                S.op("dve", lambda e: e.tensor_tensor(out=bg.ap[:, :nb, 8:16], in0=tmp8.ap[:, :nb, :],
                                                       in1=negA.ap.unsqueeze(1).to_broadcast([128, nb, 8]), op=ALU.mult), reads=[tmp8, negA], writes=[bg])
                S.dma("sp", bg_d[t0:t0 + n].rearrange("(b p) f -> p b f", p=128), bg.ap[:, :nb, :], reads=[bg])
                S.dma("sp", ab_d[t0:t0 + n].rearrange("(b p) f -> p b f", p=128), abst.ap[:, :nb, :], reads=[abst])
                S.dma("sp", pool_d[t0:t0 + n].rearrange("(b p) f -> p b f", p=128), plst.ap[:, :nb, :], reads=[plst])
            S.barrier()

    def phase_qkvpost(l):
        with ExitStack() as st:
            raw = [sb(st, "qraw%d" % i, [128, 12, NT + 2]) for i in range(2)]
            cw = sb(st, "qcw", [128, 12, 3])
            cv = sb(st, "qcv", [128, 12, NT])
            sq = sb(st, "qsq", [128, 8, NT], BF16)
            r1 = [sb(st, "qr1%d" % i, [128, NT]) for i in range(2)]
            rn = [sb(st, "qrn%d" % i, [128, NT]) for i in range(2)]
            qkT = sb(st, "qqkT", [128, 12, NT], BF16)
            kvst = sb(st, "qkvst", [128, 4, 1024], BF16)
            S.dma("sp", cw.ap, convT[l], writes=[cw])
            for ti, (t0, n, seq) in enumerate(tiles):
                nb = n // 128
                s0, s1 = (0, TL) if seq == 0 else (TL, TT)
                rw = raw[ti % 2]
                S.op("dve", lambda e: e.memset(rw.ap[:, :, 0:1], 0.0), writes=[rw])
                S.op("dve", lambda e: e.memset(rw.ap[:, :, n + 1:n + 2], 0.0), writes=[rw])
                a = max(t0 - 1, s0)
                b_ = min(t0 + n + 1, s1)
                for q3 in range(3):
                    S.dma("sp", rw.ap[:, q3 * 4:(q3 + 1) * 4, a - (t0 - 1):b_ - (t0 - 1)], qkv_d[:, q3 * 4:(q3 + 1) * 4, a:b_], writes=[rw])
                for j in range(12):
                    S.op("dve", lambda e: e.tensor_scalar(out=cv.ap[:, j, :n], in0=rw.ap[:, j, 0:n], scalar1=cw.ap[:, j, 0:1], scalar2=None, op0=ALU.mult),
                         reads=[rw, cw], writes=[cv])
                    S.op("dve", lambda e: e.scalar_tensor_tensor(out=cv.ap[:, j, :n], in0=rw.ap[:, j, 1:n + 1], scalar=cw.ap[:, j, 1:2], in1=cv.ap[:, j, :n],
                                                                  op0=ALU.mult, op1=ALU.add), reads=[rw, cw, cv], writes=[cv])
                    S.op("dve", lambda e: e.scalar_tensor_tensor(out=cv.ap[:, j, :n], in0=rw.ap[:, j, 2:n + 2], scalar=cw.ap[:, j, 2:3], in1=cv.ap[:, j, :n],
                                                                  op0=ALU.mult, op1=ALU.add), reads=[rw, cw, cv], writes=[cv])
                S.op("act", lambda e: e.activation(out=cv.ap[:, :, :n], in_=cv.ap[:, :, :n], func=AF.Silu), reads=[cv], writes=[cv])
                S.op("act", lambda e: e.activation(out=sq.ap[:, :, :n], in_=cv.ap[:, 0:8, :n], func=AF.Square), reads=[cv], writes=[sq])
                for j in range(8):
                    p = ps()
                    S.op("pe", lambda e: e.matmul(p.ap[:, :n], lhsT=onesb.ap, rhs=sq.ap[:, j, :n], start=True, stop=True), reads=[sq, onesb], writes=[p])
                    r1_ = r1[j % 2]
                    rn_ = rn[j % 2]
                    S.op("act", lambda e: e.activation(out=r1_.ap[:, :n], in_=p.ap[:, :n], func=AF.Sqrt, scale=1.0, bias=1e-6), reads=[p], writes=[r1_])
                    S.op("dve", lambda e: e.reciprocal(out=rn_.ap[:, :n], in_=r1_.ap[:, :n]), reads=[r1_], writes=[rn_])
                    sc_ = (128.0 ** -0.5) if j < 4 else 1.0
                    S.op("dve", lambda e: e.scalar_tensor_tensor(out=qkT.ap[:, j, :n], in0=cv.ap[:, j, :n], scalar=sc_, in1=rn_.ap[:, :n],
                                                                  op0=ALU.mult, op1=ALU.mult), reads=[cv, rn_], writes=[qkT])
                S.op("act", lambda e: e.activation(out=qkT.ap[:, 8:12, :n], in_=cv.ap[:, 8:12, :n], func=AF.Identity), reads=[cv], writes=[qkT])
                for b in range(nb):
                    p = ps()
                    pbf = p.ap.bitcast(BF16)
                    for hh in range(8):
                        S.op("pe", lambda e: e.transpose(pbf[:, hh * 128:(hh + 1) * 128], qkT.ap[:, 4 + hh, b * 128:(b + 1) * 128], identb.ap),
                             reads=[qkT, identb], writes=[p])
                    if b % 2 == 0:
                        S.op("act", lambda e: e.activation(out=kvst.ap[:, b, :], in_=pbf, func=AF.Identity), reads=[p], writes=[kvst])
                    else:
                        S.op("dve", lambda e: e.tensor_copy(out=kvst.ap[:, b, :], in_=pbf), reads=[p], writes=[kvst])
                S.dma("sp", qk_d[:, :, t0:t0 + n], qkT.ap[:, 0:8, :n], reads=[qkT])
                S.dma("sp", kv_d[t0:t0 + n].rearrange("(b p) f -> p b f", p=128), kvst.ap[:, :nb, :], reads=[kvst])
            S.barrier()

    def phase_delta(l):
        with ExitStack() as st:
            lt = sb(st, "dlt", [CH, 2, CH])
            negm = sb(st, "dnegm", [CH, 2, CH])
            lvm = sb(st, "dlvm", [CH, 2, 6, CH])
            S.dma("sp", lt.ap, lt_in.rearrange("d k i -> k d i"), writes=[lt])
            S.dma("sp", negm.ap, negm_in.rearrange("d k i -> k d i"), writes=[negm])
            S.dma("sp", lvm.ap, lvm_in.rearrange("d l k i -> k d l i"), writes=[lvm])
            B = []
            for d in range(2):
                t = "%d" % d
                bd = dict(
                    qk=sb(st, "dqk" + t, [128, 8, NT], BF16), kvt=sb(st, "dkvt" + t, [CH, 8, 1024], BF16), bgt=sb(st, "dbgt" + t, [CH, 8, 16]),
                    ost=sb(st, "dost" + t, [128, 4, NT]), gct=sb(st, "dgct" + t, [CH, 8, 4]), gc=sb(st, "dgc" + t, [CH, 8, 4]),
                    egc=sb(st, "degc" + t, [CH, 8, 4]), ekt=sb(st, "dekt" + t, [CH, 8, 4]), bneg=sb(st, "dbneg" + t, [CH, 8, 4]),
                    cdec=sb(st, "dcdec" + t, [128, 8, 4]), R=sb(st, "dR" + t, [CH, 4, CH]), Rb=sb(st, "dRb" + t, [CH, 4, CH]),
                    Eq=sb(st, "dEq" + t, [128, 4, CH]), Qd=sb(st, "dQd" + t, [128, 4, CH], BF16), dm=sb(st, "ddm" + t, [CH, 4, CH]),
                    Dm=sb(st, "dDm" + t, [CH, 4, CH]), attnT=sb(st, "dattn" + t, [CH, 4, CH], BF16), AT=sb(st, "dAT" + t, [CH, 4, CH]),
                    AbT=sb(st, "dAbT" + t, [CH, 6, 4, CH], BF16), T=sb(st, "dT" + t, [CH, 4, CH], BF16), TT=sb(st, "dTT" + t, [CH, 4, CH], BF16),
                    Ysb=sb(st, "dY" + t, [CH, 4, CH], BF16), Vb=sb(st, "dVb" + t, [CH, 4, 128]), Kt=sb(st, "dKt" + t, [CH, 4, 128], BF16),
                    t1=sb(st, "dt1" + t, [CH, 4, 128]), r=sb(st, "dr" + t, [CH, 4, 128], BF16), vnb=sb(st, "dvnb" + t, [CH, 4, 128], BF16),
                    S32=sb(st, "dS32" + t, [128, 4, 128]), Sbf=sb(st, "dSbf" + t, [128, 4, 128], BF16),
                )
                S.op("dve", lambda e: e.memset(bd["S32"].ap, 0.0), writes=[bd["S32"]])
                S.op("dve", lambda e: e.memset(bd["Sbf"].ap, 0.0), writes=[bd["Sbf"]])
                B.append(bd)
            lat_tiles = [t_ for t_ in tiles if t_[2] == 0]
            ctx_tile = [t_ for t_ in tiles if t_[2] == 1]
            order = [ctx_tile + lat_tiles, ctx_tile + lat_tiles[::-1]]

            def v3(ap, a):
                return ap.rearrange("p (a b) -> p a b", a=a)

            def macro_prep(d, t0, n):
                bd = B[d]
                nck = n // CH
                S.dma("sp", bd["qk"].ap[:, :, :n], qk_d[:, :, t0:t0 + n], writes=[bd["qk"]])
                S.dma("sp", bd["kvt"].ap[:, :nck, :], kv_d[t0:t0 + n].rearrange("(c p) f -> p c f", p=CH), writes=[bd["kvt"]])
                S.dma("sp", bd["bgt"].ap[:, :nck, :], bg_d[t0:t0 + n].rearrange("(c p) f -> p c f", p=CH), writes=[bd["bgt"]])
                g_ap = bd["bgt"].ap[:, :nck, 8 + d * 4:8 + (d + 1) * 4]
                be_ap = bd["bgt"].ap[:, :nck, d * 4:(d + 1) * 4]
                S.op("dve", lambda e: e.tensor_copy(out=bd["gct"].ap[:, :nck, :], in_=g_ap), reads=[bd["bgt"]], writes=[bd["gct"]])
                gflat = bd["gct"].ap[:, :nck, :].rearrange("p c h -> p (c h)")
                p = ps()
                m4 = nck * 4
                S.op("pe", lambda e: e.matmul(p.ap[0:CH, 0:m4], lhsT=lt.ap[:, d, :], rhs=gflat, start=True, stop=True), reads=[lt, bd["gct"]], writes=[p])
                S.op("pe", lambda e: e.matmul(p.ap[0:CH, 64:64 + m4], lhsT=onesf.ap[0:CH, 0:CH], rhs=gflat, start=True, stop=True), reads=[onesf, bd["gct"]], writes=[p])
                p2 = ps()
                S.op("pe", lambda e: e.matmul(p2.ap[:, 0:m4], lhsT=onesf.ap[0:CH, :], rhs=gflat, start=True, stop=True), reads=[onesf, bd["gct"]], writes=[p2])
                gcf = bd["gc"].ap[:, :nck, :].rearrange("p c h -> p (c h)")
                S.op("dve", lambda e: e.tensor_copy(out=gcf, in_=p.ap[0:CH, 0:m4]), reads=[p], writes=[bd["gc"]])
                S.op("act", lambda e: e.activation(out=bd["egc"].ap[:, :nck, :].rearrange("p c h -> p (c h)"), in_=p.ap[0:CH, 0:m4], func=AF.Exp),
                     reads=[p], writes=[bd["egc"]])
                ektf = bd["ekt"].ap[:, :nck, :].rearrange("p c h -> p (c h)")
                S.op("dve", lambda e: e.tensor_tensor(out=ektf, in0=p.ap[0:CH, 64:64 + m4], in1=gcf, op=ALU.subtract), reads=[p, bd["gc"]], writes=[bd["ekt"]])
                S.op("act", lambda e: e.activation(out=ektf, in_=ektf, func=AF.Exp), reads=[bd["ekt"]], writes=[bd["ekt"]])
                S.op("act", lambda e: e.activation(out=bd["cdec"].ap[:, :nck, :].rearrange("p c h -> p (c h)"), in_=p2.ap[:, 0:m4], func=AF.Exp),
                     reads=[p2], writes=[bd["cdec"]])
                S.op("dve", lambda e: e.scalar_tensor_tensor(out=bd["bneg"].ap[:, :nck, :], in0=be_ap, scalar=-1.0, in1=bd["egc"].ap[:, :nck, :],
                                                              op0=ALU.mult, op1=ALU.mult), reads=[bd["bgt"], bd["egc"]], writes=[bd["bneg"]])

            def chunk_pre(d, n, c):
                bd = B[d]
                cs_ = c * CH
                qk = bd["qk"]
                g_c = bd["gct"].ap[:, c, :]
                be_c = bd["bgt"].ap[:, c, d * 4:(d + 1) * 4]
                ltb = lt.ap[:, d, :].unsqueeze(1).to_broadcast([CH, 4, CH])
                S.op("dve", lambda e: e.tensor_tensor(out=bd["R"].ap, in0=ltb, in1=g_c.unsqueeze(2).to_broadcast([CH, 4, CH]), op=ALU.mult),
                     reads=[lt, bd["gct"]], writes=[bd["R"]])
                pg = ps()
                S.op("pe", lambda e: e.matmul(pg.ap[:, 0:256], lhsT=onesf.ap[0:CH, :], rhs=bd["R"].ap.rearrange("p h i -> p (h i)"), start=True, stop=True),
                     reads=[onesf, bd["R"]], writes=[pg])
                S.op("act", lambda e: e.activation(out=bd["Eq"].ap.rearrange("p h i -> p (h i)"), in_=pg.ap[:, 0:256], func=AF.Exp), reads=[pg], writes=[bd["Eq"]])
                S.op("dve", lambda e: e.tensor_tensor(out=bd["Qd"].ap, in0=qk.ap[:, 0:4, cs_:cs_ + CH], in1=bd["Eq"].ap, op=ALU.mult),
                     reads=[qk, bd["Eq"]], writes=[bd["Qd"]])
                S.op("dve", lambda e: e.tensor_tensor(out=bd["dm"].ap, in0=v3(pg.ap[0:CH, 0:256], 4), in1=bd["gc"].ap[:, c, :].unsqueeze(2).to_broadcast([CH, 4, CH]),
                                                       op=ALU.subtract), reads=[pg, bd["gc"]], writes=[bd["dm"]])
                S.op("dve", lambda e: e.scalar_tensor_tensor(out=bd["dm"].ap, in0=bd["dm"].ap, scalar=0.0, in1=negm.ap[:, d, :].unsqueeze(1).to_broadcast([CH, 4, CH]),
                                                              op0=ALU.min, op1=ALU.add), reads=[bd["dm"], negm], writes=[bd["dm"]])
                S.op("act", lambda e: e.activation(out=bd["Dm"].ap, in_=bd["dm"].ap, func=AF.Exp), reads=[bd["dm"]], writes=[bd["Dm"]])
                pk = ps()
                for hh in range(4):
                    S.op("pe", lambda e: e.matmul(pk.ap[0:CH, hh * CH:(hh + 1) * CH], lhsT=qk.ap[:, 4 + hh, cs_:cs_ + CH], rhs=qk.ap[:, 4 + hh, cs_:cs_ + CH],
                                                  start=True, stop=True), reads=[qk], writes=[pk])
                for hh in range(4):
                    S.op("pe", lambda e: e.matmul(pk.ap[0:CH, 256 + hh * CH:256 + (hh + 1) * CH], lhsT=qk.ap[:, 4 + hh, cs_:cs_ + CH], rhs=qk.ap[:, hh, cs_:cs_ + CH],
                                                  start=True, stop=True), reads=[qk], writes=[pk])
                S.op("dve", lambda e: e.tensor_tensor(out=bd["attnT"].ap, in0=v3(pk.ap[0:CH, 256:512], 4), in1=bd["Dm"].ap, op=ALU.mult),
                     reads=[pk, bd["Dm"]], writes=[bd["attnT"]])
                S.op("dve", lambda e: e.tensor_tensor(out=bd["AT"].ap, in0=v3(pk.ap[0:CH, 0:256], 4), in1=bd["Dm"].ap, op=ALU.mult),
                     reads=[pk, bd["Dm"]], writes=[bd["AT"]])
                S.op("dve", lambda e: e.tensor_tensor(out=bd["Rb"].ap, in0=ident.ap[0:CH, 0:CH].unsqueeze(1).to_broadcast([CH, 4, CH]),
                                                       in1=be_c.unsqueeze(2).to_broadcast([CH, 4, CH]), op=ALU.mult), reads=[ident, bd["bgt"]], writes=[bd["Rb"]])
                pb = ps()
                S.op("pe", lambda e: e.matmul(pb.ap[0:CH, 0:256], lhsT=onesf.ap[0:CH, 0:CH], rhs=bd["Rb"].ap.rearrange("p h i -> p (h i)"), start=True, stop=True),
                     reads=[onesf, bd["Rb"]], writes=[pb])
                S.op("dve", lambda e: e.tensor_tensor(out=bd["AT"].ap, in0=bd["AT"].ap, in1=v3(pb.ap[0:CH, 0:256], 4), op=ALU.mult),
                     reads=[bd["AT"], pb], writes=[bd["AT"]])
                S.op("dve", lambda e: e.tensor_tensor(out=bd["AbT"].ap, in0=bd["AT"].ap.unsqueeze(1).to_broadcast([CH, 6, 4, CH]),
                                                       in1=lvm.ap[:, d, :, :].unsqueeze(2).to_broadcast([CH, 6, 4, CH]), op=ALU.mult),
                     reads=[bd["AT"], lvm], writes=[bd["AbT"]])
                idb = identb.ap[0:CH, 0:CH].unsqueeze(1).to_broadcast([CH, 4, CH])
                py = ps()
                for hh in range(4):
                    S.op("pe", lambda e: e.matmul(py.ap[0:CH, hh * CH:(hh + 1) * CH], lhsT=bd["AbT"].ap[:, 0, hh, :], rhs=identb.ap[0:CH, 0:CH], start=True, stop=True),
                         reads=[bd["AbT"], identb], writes=[py])
                S.op("dve", lambda e: e.tensor_tensor(out=bd["T"].ap, in0=idb, in1=v3(py.ap[0:CH, 0:256], 4), op=ALU.subtract), reads=[identb, py], writes=[bd["T"]])
                S.op("dve", lambda e: e.tensor_tensor(out=bd["TT"].ap, in0=idb, in1=bd["AbT"].ap[:, 0], op=ALU.subtract), reads=[identb, bd["AbT"]], writes=[bd["TT"]])
                for li in range(1, 6):
                    py = ps()
                    for hh in range(4):
                        S.op("pe", lambda e: e.matmul(py.ap[0:CH, hh * CH:(hh + 1) * CH], lhsT=bd["AbT"].ap[:, li, hh, :], rhs=bd["T"].ap[:, hh, :], start=True, stop=True),
                             reads=[bd["AbT"], bd["T"]], writes=[py])
                    S.op("act", lambda e: e.activation(out=bd["Ysb"].ap, in_=v3(py.ap[0:CH, 0:256], 4), func=AF.Identity), reads=[py], writes=[bd["Ysb"]])
                    pz = ps()
                    if li < 5:
                        for hh in range(4):
                            S.op("pe", lambda e: e.matmul(pz.ap[0:CH, hh * CH:(hh + 1) * CH], lhsT=bd["TT"].ap[:, hh, :], rhs=bd["Ysb"].ap[:, hh, :], start=True, stop=True),
                                 reads=[bd["TT"], bd["Ysb"]], writes=[pz])
                    for hh in range(4):
                        S.op("pe", lambda e: e.matmul(pz.ap[0:CH, 256 + hh * CH:256 + (hh + 1) * CH], lhsT=bd["Ysb"].ap[:, hh, :], rhs=bd["TT"].ap[:, hh, :], start=True, stop=True),
                             reads=[bd["TT"], bd["Ysb"]], writes=[pz])
                    if li < 5:
                        S.op("dve", lambda e: e.tensor_tensor(out=bd["T"].ap, in0=bd["T"].ap, in1=v3(pz.ap[0:CH, 0:256], 4), op=ALU.subtract),
                             reads=[bd["T"], pz], writes=[bd["T"]])
                    S.op("dve", lambda e: e.tensor_tensor(out=bd["TT"].ap, in0=bd["TT"].ap, in1=v3(pz.ap[0:CH, 256:512], 4), op=ALU.subtract),
                         reads=[bd["TT"], pz], writes=[bd["TT"]])
                kv = bd["kvt"]
                S.op("dve", lambda e: e.tensor_tensor(out=bd["Vb"].ap, in0=v3(kv.ap[:, c, 512:1024], 4), in1=be_c.unsqueeze(2).to_broadcast([CH, 4, 128]), op=ALU.mult),
                     reads=[kv, bd["bgt"]], writes=[bd["Vb"]])
                S.op("dve", lambda e: e.tensor_tensor(out=bd["Kt"].ap, in0=v3(kv.ap[:, c, 0:512], 4), in1=bd["ekt"].ap[:, c, :].unsqueeze(2).to_broadcast([CH, 4, 128]), op=ALU.mult),
                     reads=[kv, bd["ekt"]], writes=[bd["Kt"]])

            def chunk_scan(d, n, c):
                bd = B[d]
                cs_ = c * CH
                qk = bd["qk"]
                pks = ps()
                for hh in range(4):
                    S.op("pe", lambda e: e.matmul(pks.ap[0:CH, hh * 128:(hh + 1) * 128], lhsT=qk.ap[:, 4 + hh, cs_:cs_ + CH], rhs=bd["Sbf"].ap[:, hh, :], start=True, stop=True),
                         reads=[qk, bd["Sbf"]], writes=[pks])
                S.op("dve", lambda e: e.tensor_tensor(out=bd["t1"].ap, in0=v3(pks.ap[0:CH, :], 4), in1=bd["bneg"].ap[:, c, :].unsqueeze(2).to_broadcast([CH, 4, 128]), op=ALU.mult),
                     reads=[pks, bd["bneg"]], writes=[bd["t1"]])
                S.op("dve", lambda e: e.tensor_tensor(out=bd["r"].ap, in0=bd["t1"].ap, in1=bd["Vb"].ap, op=ALU.add), reads=[bd["t1"], bd["Vb"]], writes=[bd["r"]])
                pv = ps()
                for hh in range(4):
                    S.op("pe", lambda e: e.matmul(pv.ap[0:CH, hh * 128:(hh + 1) * 128], lhsT=bd["TT"].ap[:, hh, :], rhs=bd["r"].ap[:, hh, :], start=True, stop=True),
                         reads=[bd["TT"], bd["r"]], writes=[pv])
                S.op("act", lambda e: e.activation(out=bd["vnb"].ap, in_=v3(pv.ap[0:CH, :], 4), func=AF.Identity), reads=[pv], writes=[bd["vnb"]])
                po = ps()
                for hh in range(4):
                    S.op("pe", lambda e: e.matmul(po.ap[:, hh * CH:(hh + 1) * CH], lhsT=bd["Sbf"].ap[:, hh, :], rhs=bd["Qd"].ap[:, hh, :], start=True, stop=False),
                         reads=[bd["Sbf"], bd["Qd"]], writes=[po])
                    S.op("pe", lambda e: e.matmul(po.ap[:, hh * CH:(hh + 1) * CH], lhsT=bd["vnb"].ap[:, hh, :], rhs=bd["attnT"].ap[:, hh, :], start=False, stop=True),
                         reads=[bd["vnb"], bd["attnT"]], writes=[po])
                S.op("act", lambda e: e.activation(out=bd["ost"].ap[:, :, cs_:cs_ + CH], in_=v3(po.ap[:, 0:256], 4), func=AF.Identity), reads=[po], writes=[bd["ost"]])
                psn = ps()
                for hh in range(4):
                    S.op("pe", lambda e: e.matmul(psn.ap[:, hh * 128:(hh + 1) * 128], lhsT=bd["Kt"].ap[:, hh, :], rhs=bd["vnb"].ap[:, hh, :], start=True, stop=True),
                         reads=[bd["Kt"], bd["vnb"]], writes=[psn])
                S.op("dve", lambda e: e.tensor_tensor(out=bd["S32"].ap, in0=bd["S32"].ap, in1=bd["cdec"].ap[:, c, :].unsqueeze(2).to_broadcast([128, 4, 128]), op=ALU.mult),
                     reads=[bd["S32"], bd["cdec"]], writes=[bd["S32"]])
                S.op("dve", lambda e: e.tensor_tensor(out=bd["S32"].ap, in0=bd["S32"].ap, in1=v3(psn.ap, 4), op=ALU.add), reads=[bd["S32"], psn], writes=[bd["S32"]])
                S.op("act", lambda e: e.activation(out=bd["Sbf"].ap, in_=bd["S32"].ap, func=AF.Identity), reads=[bd["S32"]], writes=[bd["Sbf"]])

            for ms in range(len(order[0])):
                info = []
                for d in range(2):
                    t0, n, seq = order[d][ms]
                    macro_prep(d, t0, n)
                    info.append((t0, n))
                nck = info[0][1] // CH
                for ci in range(nck):
                    cidx = [ci, nck - 1 - ci]
                    for d in range(2):
                        chunk_pre(d, info[d][1], cidx[d])
                    for d in range(2):
                        chunk_scan(d, info[d][1], cidx[d])
                for d in range(2):
                    t0, n = info[d]
                    S.dma("sp", o_d[d, :, :, t0:t0 + n], B[d]["ost"].ap[:, :, :n], reads=[B[d]["ost"]])
            S.barrier()

    def phase_dft(l):
        with ExitStack() as st:
            ABs = sb(st, "fab", [128, TL // 128, 512], BF16)
            dcs = [sb(st, "fdc%d" % i, [128, 8, NT], BF16) for i in range(4)]
            fst = [sb(st, "ffst%d" % i, [128, 2, NT], BF16) for i in range(2)]
            di = 0
            fi = 0
            for (s0, T, dsrc) in ((0, TL, WB["dftl"]), (TL, TC, WB["dftc"])):
                ntc = T // 128
                for c8 in range(0, ntc, 8):
                    c9 = min(ntc, c8 + 8)
                    S.dma("sp", ABs.ap[:, c8:c9, :], ab_d[s0 + c8 * 128:s0 + c9 * 128].rearrange("(c p) f -> p c f", p=128), writes=[ABs])
                kn = min(NT, T)
                for kt0 in range(0, T, kn):
                    pf = [ps(), ps()]
                    ngrp = (ntc + 7) // 8
                    for tg in range(ngrp):
                        nc8 = min(8, ntc - tg * 8)
                        dd = []
                        for cs_ in range(2):
                            dbuf = dcs[di % 4]
                            di += 1
                            S.dma("sp", dbuf.ap[:, :nc8, :kn], dsrc.ap[cs_, tg * 1024:tg * 1024 + nc8 * 128, kt0:kt0 + kn].rearrange("(c p) k -> p c k", p=128), reads=[dsrc], writes=[dbuf])
                            dd.append(dbuf)
                        for c in range(nc8):
                            tch = tg * 8 + c
                            for fc in range(2):
                                first = (tch == 0)
                                last_ = (tch == ntc - 1)
                                S.op("pe", lambda e: e.matmul(pf[fc].ap[:, :kn], lhsT=ABs.ap[:, tch, fc * 256:fc * 256 + 128], rhs=dd[0].ap[:, c, :kn], start=first, stop=False),
                                     reads=[ABs, dd[0]], writes=[pf[fc]])
                                S.op("pe", lambda e: e.matmul(pf[fc].ap[:, :kn], lhsT=ABs.ap[:, tch, fc * 256 + 128:fc * 256 + 256], rhs=dd[1].ap[:, c, :kn], start=False, stop=last_),
                                     reads=[ABs, dd[1]], writes=[pf[fc]])
                    fs = fst[fi % 2]
                    fi += 1
                    S.op("act", lambda e: e.activation(out=fs.ap[:, 0, :kn], in_=pf[0].ap[:, :kn], func=AF.Identity), reads=[pf[0]], writes=[fs])
                    S.op("dve", lambda e: e.tensor_copy(out=fs.ap[:, 1, :kn], in_=pf[1].ap[:, :kn]), reads=[pf[1]], writes=[fs])
                    S.dma("sp", f_d[:, :, s0 + kt0:s0 + kt0 + kn], fs.ap[:, :, :kn], reads=[fs])
            S.barrier()

    def phase_merge(l, do_ctx):
        with ExitStack() as st:
            hx = sb(st, "mhx", [128, KC, NT], BF16)
            xt = sb(st, "mxt", [128, KC, NT])
            Y = [sb(st, "mY%d" % i, [128, KC, NT]) for i in range(3)]
            of_ = sb(st, "mof", [128, 4, NT])
            ob_ = sb(st, "mob", [128, 4, NT])
            zs = sb(st, "mzs", [128, 4, NT])
            osq = sb(st, "mosq", [128, 4, NT], BF16)
            ofb = sb(st, "mofb", [128, 4, NT], BF16)
            r1 = sb(st, "mr1", [128, NT])
            rn = sb(st, "mrn", [128, NT])
            og = sb(st, "mog", [128, 1])
            psc = sb(st, "mpsc", [128, KC])
            wa = sb(st, "mwa", [128, 4, D], BF16)
            wb = sb(st, "mwb", [CH, 4, 256], BF16)
            wc = sb(st, "mwc", [128, 2, D], BF16)
            wg = [sb(st, "mwg%d" % i, [128, KC, 512], BF16) for i in range(2)]
            pin = sb(st, "mpin", [128, 12, 256], BF16)
            pbk = [sb(st, "mpbk%d" % i, [128, NT], BF16) for i in range(4)]
            pT = [sb(st, "mpT%d" % i, [CH, NT], BF16) for i in range(4)]
            fT = sb(st, "mfT", [128, 2, NT], BF16)
            gt = [sb(st, "mgt%d" % i, [128, NT]) for i in range(2)]
            mb = sb(st, "mmb", [128, KC, NT], BF16)
            S.dma("sp", og.ap, ogT[l], writes=[og])
            S.dma("sp", psc.ap, pscT[l], writes=[psc])
            win = WB[("in", l)]
            wob = WB[("out", l)]
            for hh in range(4):
                S.dma("sp", wa.ap[:, hh, :], WB[("a", l)].ap[hh * 128:(hh + 1) * 128, :], reads=[WB[("a", l)]], writes=[wa])
            S.dma("sp", wb.ap, WB[("b", l)].ap.rearrange("g c d -> c g d"), reads=[WB[("b", l)]], writes=[wb])
            for c in range(2):
                S.dma("sp", wc.ap[:, c, :], WB[("c", l)].ap[c * 128:(c + 1) * 128, :], reads=[WB[("c", l)]], writes=[wc])
            wgi = 0
            pbi = 0
            for ti, (t0, n, seq) in enumerate(tiles):
                if seq == 1 and not do_ctx:
                    continue
                tile_idx = ti if seq == 0 else 0
                S.dma("sp", hx.ap[:, :, :n], hx_d[:, :, t0:t0 + n], writes=[hx])
                S.dma("sp", xt.ap[:, :, :n], xs_d[:, :, t0:t0 + n], writes=[xt])
                S.dma("sp", of_.ap[:, :, :n], o_d[0, :, :, t0:t0 + n], writes=[of_])
                S.dma("sp", ob_.ap[:, :, :n], o_d[1, :, :, t0:t0 + n], writes=[ob_])
                S.dma("sp", zs.ap[:, :, :n], zs_d[:, :, t0:t0 + n], writes=[zs])
                S.dma("sp", fT.ap[:, :, :n], f_d[:, :, t0:t0 + n], writes=[fT])
                S.op("dve", lambda e: e.tensor_tensor(out=of_.ap[:, :, :n], in0=of_.ap[:, :, :n], in1=ob_.ap[:, :, :n], op=ALU.add), reads=[of_, ob_], writes=[of_])
                S.op("act", lambda e: e.activation(out=osq.ap[:, :, :n], in_=of_.ap[:, :, :n], func=AF.Square), reads=[of_], writes=[osq])
                for hh in range(4):
                    p = ps()
                    S.op("pe", lambda e: e.matmul(p.ap[:, :n], lhsT=onesb.ap, rhs=osq.ap[:, hh, :n], start=True, stop=True), reads=[onesb, osq], writes=[p])
                    S.op("act", lambda e: e.activation(out=r1.ap[:, :n], in_=p.ap[:, :n], func=AF.Sqrt, scale=1.0 / 128, bias=1e-6), reads=[p], writes=[r1])
                    S.op("dve", lambda e: e.reciprocal(out=rn.ap[:, :n], in_=r1.ap[:, :n]), reads=[r1], writes=[rn])
                    S.op("dve", lambda e: e.scalar_tensor_tensor(out=of_.ap[:, hh, :n], in0=of_.ap[:, hh, :n], scalar=og.ap[:, 0:1], in1=rn.ap[:, :n],
                                                                  op0=ALU.mult, op1=ALU.mult), reads=[of_, og, rn], writes=[of_])
                S.op("dve", lambda e: e.tensor_tensor(out=ofb.ap[:, :, :n], in0=of_.ap[:, :, :n], in1=zs.ap[:, :, :n], op=ALU.mult), reads=[of_, zs], writes=[ofb])
                for m in range(KC):
                    p = ps()
                    for hh in range(4):
                        S.op("pe", lambda e: e.matmul(p.ap[:, :n], lhsT=wa.ap[:, hh, m * 128:(m + 1) * 128], rhs=ofb.ap[:, hh, :n], start=(hh == 0), stop=(hh == 3)),
                             reads=[wa, ofb], writes=[p])
                    S.op("act", lambda e: e.activation(out=Y[0].ap[:, m, :n], in_=p.ap[:, :n], func=AF.Identity), reads=[p], writes=[Y[0]])
                allsrc = sorted(set(sc for g in range(4) for (sc, bi, ncol) in plan[(seq, tile_idx, g)]))
                lo, hi = allsrc[0], allsrc[-1] + 128
                nsrc = (hi - lo) // 128
                for c6 in range(0, nsrc, 6):
                    c7 = min(nsrc, c6 + 6)
                    S.dma("sp", pin.ap[:, c6:c7, :], pool_d[lo + c6 * 128:lo + c7 * 128].rearrange("(c p) f -> p c f", p=128), writes=[pin])
                for g in range(4):
                    p = ps()
                    lst = plan[(seq, tile_idx, g)]
                    for ii, (sc, bi, ncol) in enumerate(lst):
                        pb_ = pbk[pbi % 4]
                        pbi += 1
                        S.dma("sp", pb_.ap, pblk_b[bi], reads=[WB["pblk"]], writes=[pb_])
                        sci = (sc - lo) // 128
                        S.op("pe", lambda e: e.matmul(p.ap[0:CH, :n], lhsT=pin.ap[:, sci, g * CH:(g + 1) * CH], rhs=pb_.ap[:, :n], start=(ii == 0), stop=(ii == len(lst) - 1)),
                             reads=[pin, pb_], writes=[p])
                    S.op("act", lambda e: e.activation(out=pT[g].ap[:, :n], in_=p.ap[0:CH, :n], func=AF.Identity), reads=[p], writes=[pT[g]])
                    for dc in range(2):
                        p2 = ps()
                        S.op("pe", lambda e: e.matmul(p2.ap[:, :n], lhsT=wb.ap[:, g, dc * 128:(dc + 1) * 128], rhs=pT[g].ap[:, :n], start=True, stop=True),
                             reads=[wb, pT[g]], writes=[p2])
                        S.op("dve", lambda e: e.tensor_scalar(out=Y[1].ap[:, g * 2 + dc, :n], in0=p2.ap[:, :n], scalar1=psc.ap[:, g * 2 + dc:g * 2 + dc + 1], scalar2=None, op0=ALU.mult),
                             reads=[p2, psc], writes=[Y[1]])
                for m in range(KC):
                    p = ps()
                    for c in range(2):
                        S.op("pe", lambda e: e.matmul(p.ap[:, :n], lhsT=wc.ap[:, c, m * 128:(m + 1) * 128], rhs=fT.ap[:, c, :n], start=(c == 0), stop=(c == 1)),
                             reads=[wc, fT], writes=[p])
                    S.op("dve", lambda e: e.tensor_copy(out=Y[2].ap[:, m, :n], in_=p.ap[:, :n]), reads=[p], writes=[Y[2]])
                for br in range(3):
                    for half in range(2):
                        w = wg[wgi % 2]
                        wgi += 1
                        c0 = 2576 + br * D + half * 512
                        S.dma("sp", w.ap, win.ap[:, c0:c0 + 512].rearrange("(k p) c -> p k c", p=128), reads=[win], writes=[w])
                        for kk in range(4):
                            k = half * 4 + kk
                            p = ps()
                            for kq in range(KC):
                                S.op("pe", lambda e: e.matmul(p.ap[:, :n], lhsT=w.ap[:, kq, kk * 128:(kk + 1) * 128], rhs=hx.ap[:, kq, :n], start=(kq == 0), stop=(kq == KC - 1)),
                                     reads=[w, hx], writes=[p])
                            g_ = gt[k % 2]
                            S.op("act", lambda e: e.activation(out=g_.ap[:, :n], in_=p.ap[:, :n], func=AF.Sigmoid), reads=[p], writes=[g_])
                            S.op("dve", lambda e: e.tensor_tensor(out=Y[br].ap[:, k, :n], in0=Y[br].ap[:, k, :n], in1=g_.ap[:, :n], op=ALU.mult),
                                 reads=[Y[br], g_], writes=[Y[br]])
                S.op("dve", lambda e: e.tensor_tensor(out=Y[0].ap[:, :, :n], in0=Y[0].ap[:, :, :n], in1=Y[1].ap[:, :, :n], op=ALU.add), reads=[Y[0], Y[1]], writes=[Y[0]])
                S.op("dve", lambda e: e.tensor_tensor(out=mb.ap[:, :, :n], in0=Y[0].ap[:, :, :n], in1=Y[2].ap[:, :, :n], op=ALU.add), reads=[Y[0], Y[2]], writes=[mb])
                for half in range(2):
                    w = wg[wgi % 2]
                    wgi += 1
                    S.dma("sp", w.ap, wob.ap[:, half * 512:(half + 1) * 512].rearrange("(k p) c -> p k c", p=128), reads=[wob], writes=[w])
                    for mm in range(4):
                        m = half * 4 + mm
                        p = ps()
                        for k in range(KC):
                            S.op("pe", lambda e: e.matmul(p.ap[:, :n], lhsT=w.ap[:, k, mm * 128:(mm + 1) * 128], rhs=mb.ap[:, k, :n], start=(k == 0), stop=(k == KC - 1)),
                                 reads=[w, mb], writes=[p])
                        S.op("dve", lambda e: e.scalar_tensor_tensor(out=xt.ap[:, m, :n], in0=p.ap[:, :n], scalar=Gmod.ap[:, l, 1, m, seq:seq + 1], in1=xt.ap[:, m, :n],
                                                                      op0=ALU.mult, op1=ALU.add), reads=[p, xt, Gmod], writes=[xt])
                S.dma("sp", xs_d[:, :, t0:t0 + n], xt.ap[:, :, :n], reads=[xt])
            S.barrier()

    for l in range(DEPTH):
        conv_layer(l)
    phase_xin()
    for l in range(DEPTH):
        last = l == DEPTH - 1
        phase_mods(l)
        if stages >= 1:
            phase_ffn(l, 0, WB[("g1", l)], WB[("d1", l)], True)
        if stages >= 2:
            if upto >= 1:
                phase_proj(l)
            if upto >= 2:
                phase_qkvpost(l)
            if upto >= 3:
                phase_delta(l)
            if upto >= 4:
                phase_dft(l)
            if upto >= 5:
                phase_merge(l, not last)
        if stages >= 3:
            phase_ffn(l, 2, WB[("g2", l)], WB[("d2", l)], not last)
    phase_final()
    S.barrier()
    S.emit()
    gstack.close()
    return nc, S


def _prep_shared(inp, TL, TC, DEPTH):
    f = lambda a: np.ascontiguousarray(np.asarray(a, dtype=np.float32))
    consts, plan = host_consts(TL, TC)
    sh = {}
    sh["w_ada"] = f(inp["w_ada"])
    sh["b_adaT"] = f(np.asarray(inp["b_ada"]).reshape(DEPTH, 72, 128).transpose(0, 2, 1))
    sh["gainT"] = f(np.asarray(inp["norm_gain"]).reshape(DEPTH, 3, KC, 128).transpose(0, 3, 1, 2))
    sh["fgainT"] = f(np.asarray(inp["final_gain"]).reshape(KC, 128).T)
    for k in ("w_ffn1_gu", "w_ffn1_down", "w_ffn2_gu", "w_ffn2_down", "w_in", "w_a", "w_b", "w_c", "w_out"):
        sh[k] = f(inp[k])
    sh["convT"] = f(np.asarray(inp["conv_w"]).reshape(DEPTH, 3, 12, 128).transpose(0, 3, 2, 1))
    sh["alogB"] = f(np.broadcast_to(np.asarray(inp["a_log"]).reshape(DEPTH, 1, 8), (DEPTH, 128, 8)))
    sh["dtbB"] = f(np.broadcast_to(np.asarray(inp["dt_bias"]).reshape(DEPTH, 1, 8), (DEPTH, 128, 8)))
    sh["ogT"] = f(np.asarray(inp["o_gain"]).reshape(DEPTH, 128, 1))
    sh["pscT"] = f(np.asarray(inp["pool_scale"]).reshape(DEPTH, KC, 128).transpose(0, 2, 1))
    for k, v in consts.items():
        sh[k] = f(v)
    return sh, plan


def run(inp, TL, TC, DEPTH, n_cores, debug=False, stages=99, trace=False, upto=99):
    import time as _t
    _t0 = _t.time()
    sh, plan = _prep_shared(inp, TL, TC, DEPTH)
    print("[kernel] prep %.1fs" % (_t.time() - _t0), flush=True)
    _t0 = _t.time()
    nc, S = build_program(TL, TC, DEPTH, debug=debug, plan=plan, npblk=sh["pblk"].shape[0], stages=stages, upto=upto)
    print("[kernel] build %.1fs nins=%d nwait=%d sems=%d" % (_t.time() - _t0, S.nins, S.nwait, len(S.sems)), flush=True)
    x = np.asarray(inp["x"], dtype=np.float32)
    c = np.asarray(inp["c"], dtype=np.float32)
    ctx = np.asarray(inp["ctx"], dtype=np.float32)
    c_ctx = np.asarray(inp["c_ctx"], dtype=np.float32)
    in_maps = []
    for b in range(n_cores):
        m = dict(sh)
        m["x"] = np.ascontiguousarray(x[b])
        m["ctx"] = np.ascontiguousarray(ctx[b])
        cc = np.stack([c[b], c_ctx], axis=-1)
        m["cT"] = np.ascontiguousarray(cc.reshape(KC, 128, 2).transpose(1, 0, 2))
        in_maps.append(m)
    _t0 = _t.time()
    res = run_bass_kernel_spmd(nc, in_maps, core_ids=list(range(n_cores)), **({"trace": True} if trace else {}))
    print("[kernel] run %.1fs" % (_t.time() - _t0), flush=True)
    return res


def kernel(**inputs):
    res = run(inputs, 4096, 256, 4, 8)
    return np.stack([np.asarray(r["out"], dtype=np.float32) for r in res.results], axis=0)
```
